# Optimizing a Trainium2 kernel written in Bass

```python
import jax, jax.numpy as jnp
from jax import lax
import numpy as np

D_MODEL = 1024
BATCH = 8
SEQ = 4096
DEPTH = 1

HEAD_DIM = 64
N_HEADS = 8
N_KV_HEADS = 2
GROUP = N_HEADS // N_KV_HEADS
WINDOW = 128
BLOCK = 128
ATTN_SCALE = HEAD_DIM ** -0.5
ATTN_WIDTH = N_HEADS * HEAD_DIM
KV_WIDTH = N_KV_HEADS * HEAD_DIM
CONV_GROUPS = 8
CONV_WIDTH = CONV_GROUPS * 64
CONV_K = 3
IN_WIDTH = ATTN_WIDTH + 2 * KV_WIDTH + 3 * CONV_WIDTH + 2 * D_MODEL
D_FF = 2816
FFN_CONV_K = 3
NORM_EPS = 1e-5

kernel_name = "hybrid_swa_sink_shortconv_gated_convffn"


def rms_norm(x, g):
    xf = x.astype(jnp.float32)
    y = xf * lax.rsqrt(jnp.mean(xf * xf, axis=-1, keepdims=True) + NORM_EPS)
    return (y * g.astype(jnp.float32)).astype(x.dtype)


def causal_depthwise_conv(x, w):
    k_width, ch = w.shape
    return lax.conv_general_dilated(
        x, w[:, None, :].astype(x.dtype), window_strides=(1,), padding=((k_width - 1, 0),),
        dimension_numbers=("NWC", "WIO", "NWC"), feature_group_count=ch)


def sliding_window_attention(q, k, v, sinks):
    b, s, _ = q.shape
    nb = s // BLOCK
    q = q.reshape(b, nb, BLOCK, N_KV_HEADS, GROUP, HEAD_DIM)
    k = k.reshape(b, nb, BLOCK, N_KV_HEADS, HEAD_DIM)
    v = v.reshape(b, nb, BLOCK, N_KV_HEADS, HEAD_DIM)

    def with_prev(t):
        prev = jnp.pad(t, ((0, 0), (1, 0), (0, 0), (0, 0), (0, 0)))[:, :-1]
        return jnp.concatenate([prev, t], axis=2)

    kw, vw = with_prev(k), with_prev(v)
    scores = jnp.einsum("bnqhgd,bnkhd->bnhgqk", q, kw).astype(jnp.float32) * ATTN_SCALE
    qi = jnp.arange(BLOCK)[:, None]
    kj = jnp.arange(2 * BLOCK)[None, :]
    dist = qi + BLOCK - kj
    band = (dist >= 0) & (dist < WINDOW)
    real = (jnp.arange(nb)[:, None, None] > 0) | (kj[None] >= BLOCK)
    mask = band[None] & real
    scores = jnp.where(mask[None, :, None, None], scores, -jnp.inf)
    sink = jnp.broadcast_to(sinks.astype(jnp.float32).reshape(1, 1, N_KV_HEADS, GROUP, 1, 1),
                            scores.shape[:-1] + (1,))
    probs = jax.nn.softmax(jnp.concatenate([scores, sink], axis=-1), axis=-1)[..., :-1]
    out = jnp.einsum("bnhgqk,bnkhd->bnqhgd", probs.astype(v.dtype), vw)
    return out.reshape(b, s, ATTN_WIDTH)


def setup_inputs(seed: int = 0) -> dict:
    key = jax.random.key(seed)
    ks = jax.random.split(key, 16)
    f32 = jnp.float32

    def nrm(k, shape, scale):
        return jax.random.normal(k, shape, f32) * scale

    return {
        "x": nrm(ks[0], (BATCH, SEQ, D_MODEL), 1.0),
        "mix_norm": 1.0 + nrm(ks[1], (DEPTH, D_MODEL), 0.02),
        "w_in": nrm(ks[2], (DEPTH, D_MODEL, IN_WIDTH), D_MODEL ** -0.5),
        "b_in": nrm(ks[3], (DEPTH, IN_WIDTH), 0.02),
        "sinks": nrm(ks[4], (DEPTH, N_HEADS), 0.5),
        "conv_w": nrm(ks[5], (DEPTH, CONV_K, CONV_WIDTH), CONV_K ** -0.5),
        "w_attn_branch": nrm(ks[6], (DEPTH, ATTN_WIDTH, D_MODEL), ATTN_WIDTH ** -0.5),
        "w_conv_branch": nrm(ks[7], (DEPTH, CONV_WIDTH, D_MODEL), CONV_WIDTH ** -0.5),
        "w_out": nrm(ks[8], (DEPTH, D_MODEL, D_MODEL), D_MODEL ** -0.5),
        "ffn_norm": 1.0 + nrm(ks[9], (DEPTH, D_MODEL), 0.02),
        "w_up": nrm(ks[10], (DEPTH, D_MODEL, 2 * D_FF), D_MODEL ** -0.5),
        "ffn_conv_w": nrm(ks[11], (DEPTH, FFN_CONV_K, 2 * D_FF), FFN_CONV_K ** -0.5),
        "w_down": nrm(ks[12], (DEPTH, D_FF, D_MODEL), D_FF ** -0.5),
        "final_norm": 1.0 + nrm(ks[13], (D_MODEL,), 0.02),
    }


def reference(x, mix_norm, w_in, b_in, sinks, conv_w, w_attn_branch, w_conv_branch, w_out,
              ffn_norm, w_up, ffn_conv_w, w_down, final_norm):
    h = x
    splits = np.cumsum([ATTN_WIDTH, KV_WIDTH, KV_WIDTH, CONV_WIDTH, CONV_WIDTH, CONV_WIDTH, D_MODEL])
    for l in range(DEPTH):
        xn = rms_norm(h, mix_norm[l])
        proj = jnp.einsum("bsd,dp->bsp", xn, w_in[l]) + b_in[l]
        q, k, v, cb, cc, cx, ga, gc = jnp.split(proj, splits, axis=-1)
        attn = sliding_window_attention(q, k, v, sinks[l])
        conv = cb * causal_depthwise_conv(cc * cx, conv_w[l])
        merged = (jax.nn.sigmoid(ga) * jnp.einsum("bsc,cd->bsd", attn, w_attn_branch[l])
                  + jax.nn.sigmoid(gc) * jnp.einsum("bsc,cd->bsd", conv, w_conv_branch[l]))
        h = h + jnp.einsum("bsd,de->bse", merged, w_out[l])
        hn = rms_norm(h, ffn_norm[l])
        up = causal_depthwise_conv(jnp.einsum("bsd,df->bsf", hn, w_up[l]), ffn_conv_w[l])
        gate, val = jnp.split(up, 2, axis=-1)
        h = h + jnp.einsum("bsf,fd->bsd", jax.nn.silu(gate) * val, w_down[l])
    return rms_norm(h, final_norm)
```

```python
import numpy as np
import ml_dtypes
from contextlib import ExitStack
import concourse.bass as bass
import concourse.mybir as mybir
from concourse.bass_utils import run_bass_kernel_spmd

F32 = mybir.dt.float32
BF16 = mybir.dt.bfloat16
AF = mybir.ActivationFunctionType
ALU = mybir.AluOpType

D = 1024
T = 512
NSUB = 4
NI_MIX = 42
NI_FFN = 44
NI = NI_MIX + NI_FFN
NPAIR = NI // 2
NRES = 30
RS = 4
NTMP = 8
NXSB = 4
EPS = 1e-5

C_BIAS = 0
C_G1 = 34
C_G2 = 42
C_CW = 50
C_FW = 62
C_SK = 194
C_BV = 198
C_GF = 326
NCST = 1350


class Buf:
    def __init__(self, name, exclusive=False):
        self.name = name
        self.exclusive = exclusive
        self.last_w = None
        self.readers = []


class Chan:
    def __init__(self, name):
        self.name = name
        self.sem = None
        self.count = 0


class Op:
    __slots__ = ("eng", "fn", "deps", "signal", "sigval", "chan", "chanval", "idx")

    def __init__(self, eng, fn, chan=None):
        self.eng = eng
        self.fn = fn
        self.deps = []
        self.signal = False
        self.sigval = 0
        self.chan = chan
        self.chanval = 0
        self.idx = 0


ENGINES = ("pe", "act", "dve", "pool", "sp")
NOSELF = ("pe", "sp")


class Sched:
    def __init__(self):
        self.ops = {e: [] for e in ENGINES}
        self.chans = []

    def chan(self, name):
        c = Chan(name)
        self.chans.append(c)
        return c

    def add(self, eng, fn, reads=(), writes=(), chan=None, extra=()):
        op = Op(eng, fn, chan)
        deps = {}

        def need(o):
            if o is None or o is op:
                return
            deps[id(o)] = o

        for b in reads:
            need(b.last_w)
            if b.exclusive:
                for r in b.readers:
                    if r.eng != eng:
                        need(r)
        for b in writes:
            need(b.last_w)
            for r in b.readers:
                need(r)
        for o in extra:
            need(o)
        for b in reads:
            b.readers.append(op)
        for b in writes:
            b.last_w = op
            b.readers = []
        if chan is not None:
            chan.count += 1
            op.chanval = 16 * chan.count
        op.idx = len(self.ops[eng])
        best = {}
        out = []
        for o in deps.values():
            if o.chan is not None:
                out.append(o)
                continue
            if o.eng == eng and eng in NOSELF:
                continue
            cur = best.get(o.eng)
            if cur is None or o.idx > cur.idx:
                best[o.eng] = o
        out.extend(best.values())
        op.deps = out
        self.ops[eng].append(op)
        return op

    def run(self, nc, sems):
        for e in ENGINES:
            for op in self.ops[e]:
                for d in op.deps:
                    if d.chan is None:
                        d.signal = True
        for e in ENGINES:
            n = 0
            for op in self.ops[e]:
                if op.chan is None and op.signal:
                    n += 1
                    op.sigval = n

        def replay(name, e):
            waited = {}
            for op in self.ops[name]:
                for d in op.deps:
                    if d.chan is not None:
                        key, sem, val = ("c", id(d.chan)), d.chan.sem, d.chanval
                    else:
                        key, sem, val = ("e", d.eng), sems[d.eng], d.sigval
                    if waited.get(key, 0) >= val:
                        continue
                    e.wait_ge(sem, val)
                    waited[key] = val
                ins = op.fn(e)
                if op.chan is not None:
                    ins.then_inc(op.chan.sem, 16)
                elif op.signal:
                    ins.then_inc(sems[name], 1)

        with nc.Block() as block:
            @block.tensor
            def _(e):
                replay("pe", e)

            @block.scalar
            def _(e):
                replay("act", e)

            @block.vector
            def _(e):
                replay("dve", e)

            @block.gpsimd
            def _(e):
                replay("pool", e)

            @block.sync
            def _(e):
                replay("sp", e)


def item_table():
    items = []
    nb = 0
    for g in range(4):
        items.append(("q", g, nb)); nb += 1
    items.append(("k", 0, nb)); nb += 1
    items.append(("v", 0, nb)); nb += 1
    for c in range(4):
        for kind in ("cx", "cc", "cb"):
            items.append((kind, c, nb)); nb += 1
    for c in range(8):
        items.append(("ga", c, nb)); nb += 1
        items.append(("gc", c, nb)); nb += 1
        items.append(("wac", c, -1))
    assert nb == 34 and len(items) == NI_MIX
    for j in range(22):
        items.append(("fg", j, -1))
        items.append(("fv", j, -1))
    assert len(items) == NI
    return items


ITEMS = item_table()


def build_nc(ntiles):
    S_TOK = ntiles * T
    nc = bass.Bass("TRN2", target_bir_lowering=False)
    x_d = nc.dram_tensor("x", [S_TOK, D], F32, kind="ExternalInput").ap()
    wst_d = nc.dram_tensor("wst", [NI, 128, 1024], F32, kind="ExternalInput").ap()
    wres_d = nc.dram_tensor("wres", [NRES, 128, 1024], F32, kind="ExternalInput").ap()
    cst_d = nc.dram_tensor("cst", [128, NCST], F32, kind="ExternalInput").ap()
    cbf_d = nc.dram_tensor("cbf", [128, 384], BF16, kind="ExternalInput").ap()
    out_d = nc.dram_tensor("out", [S_TOK, D], F32, kind="ExternalOutput").ap()
    wsc_d = nc.dram_tensor("wsc", [NI, 128, 1024], BF16).ap()

    S = Sched()
    with ExitStack() as ctx:
        def sb(name, shape, dt):
            return ctx.enter_context(nc.sbuf_tensor("s_" + name, shape, dt))

        def newsem(name):
            return ctx.enter_context(nc.semaphore(name))

        def chan(name):
            c = S.chan(name)
            c.sem = newsem("c_" + name)
            return c

        cst = sb("cst", [128, NCST], F32)
        cbf = sb("cbf", [128, 384], BF16)
        wo_r = sb("wo_r", [128, 8, 1024], BF16)
        wd_r = sb("wd_r", [128, 22, 1024], BF16)
        ring = [sb("ring%d" % i, [128, 2, 1024], BF16) for i in range(RS)]
        io = [sb("io%d" % i, [128, 1024], F32) for i in range(3)]
        xsb = [sb("xsb%d" % i, [128, 1024], BF16) for i in range(NXSB)]
        nT = sb("nT", [128, 8, T], BF16)
        QT = sb("QT", [128, 4, T], BF16)
        KT = sb("KT", [128, 128 + T], BF16)
        Vt = sb("Vt", [128, 5, 128], BF16)
        ccx = sb("ccx", [128, 4, T + 2], F32)
        AC = sb("AC", [128, 8, T], BF16)
        PT = [sb("PT%d" % i, [128, 4 * 512], BF16) for i in range(2)]
        mergedT = sb("mergedT", [128, 8, T], BF16)
        xh = sb("xh", [128, NSUB, D], F32)
        tmps = [sb("tmp%d" % i, [128, 512], F32) for i in range(NTMP)]
        actT = sb("actT", [128, 22, T], BF16)
        uh = sb("uh", [128, 44, 2], F32)
        fx = sb("fx", [128, 44, 2], F32)
        fxt = sb("fxt", [128, 44], F32)
        ss = sb("ss", [128, 16], F32)
        rs = sb("rs", [128, 16], F32)
        sk = sb("sk", [128, 4], F32)
        mhalf = sb("mhalf", [128, 1], F32)
        dummy = sb("dummy", [128, 2], F32)
        ones64 = sb("ones64", [128, 64], BF16)
        ps = ctx.enter_context(nc.psum_tensor("ps", [128, 8 * 512], F32))
        attnT = AC[:, 0:4, :]
        convT = AC[:, 4:8, :]
        hnT = AC

        sems = {e: newsem("sem_" + e) for e in ENGINES}

        b_cst = Buf("cst"); b_cbf = Buf("cbf")
        b_wo = [Buf("wo%d" % i) for i in range(8)]
        b_wd = [Buf("wd%d" % i) for i in range(22)]
        b_ring = [Buf("ring%d" % i) for i in range(RS)]
        c_ring = [chan("ring%d" % i) for i in range(RS)]
        b_io = [Buf("io%d" % i) for i in range(3)]
        c_io_in = [chan("ioin%d" % i) for i in range(3)]
        c_io_out = [chan("ioout%d" % i) for i in range(3)]
        b_xsb = [Buf("xsb%d" % i) for i in range(NXSB)]
        b_nT = [Buf("nT%d" % i) for i in range(NSUB)]
        b_hnT = [Buf("hnT%d" % i) for i in range(NSUB)]
        b_QT = [Buf("QT%d" % i) for i in range(4)]
        b_KT = Buf("KT"); b_V = Buf("V")
        b_ccx = [Buf("ccx%d" % i) for i in range(4)]
        b_convT = [Buf("convT%d" % i) for i in range(4)]
        b_PT = [[[Buf("PT%d_%d_%d" % (i, p, k)) for k in range(2)] for p in range(2)] for i in range(2)]
        b_attnT = [Buf("attnT%d" % i) for i in range(4)]
        b_merged = [Buf("merged%d" % i) for i in range(8)]
        b_xh = [Buf("xh%d" % i) for i in range(NSUB)]
        c_xh = [chan("xh%d" % i) for i in range(NSUB)]
        b_tmp = [Buf("tmp%d" % i) for i in range(NTMP)]
        b_actT = [Buf("actT%d" % i) for i in range(22)]
        b_uh = Buf("uh"); b_fx = Buf("fx"); b_fxt = Buf("fxt")
        b_ss = [Buf("ss%d" % i) for i in range(16)]
        b_rs = [Buf("rs%d" % i) for i in range(16)]
        b_sk = Buf("sk"); b_small = Buf("small"); b_dummy = Buf("dummy")
        banks = [Buf("bank%d" % i, exclusive=True) for i in range(8)]
        b_wsc = [Buf("wsc%d" % i) for i in range(NI)]
        c_cst = chan("cst")
        c_cbf = chan("cbf")
        b_ctok = [Buf("ctok%d" % i) for i in range(NSUB)]
        hn_alias = b_attnT + b_ctok

        def hn_al(s):
            return [b_attnT[s], b_ctok[s]]

        state = {"bank": 0, "tmp": 0, "io": 0, "xsb": 0}

        def nextbank():
            i = state["bank"]
            state["bank"] = (i + 1) % 8
            return i

        def bank_ap(i):
            return ps[:, i * 512:(i + 1) * 512]

        def nexttmp():
            i = state["tmp"]
            state["tmp"] = (i + 1) % NTMP
            return tmps[i], b_tmp[i]

        def nextio():
            i = state["io"]
            state["io"] = (i + 1) % 3
            return i

        def nextxsb():
            i = state["xsb"]
            state["xsb"] = (i + 1) % NXSB
            return i

        def cc(col, n=1):
            return cst[:, col:col + n]

        ident = cbf[:, 0:128]
        maskp = cbf[:, 128:256]
        maskc = cbf[:, 256:384]

        S.add("sp", lambda e: e.dma_start(out=cst[:], in_=cst_d), writes=[b_cst], chan=c_cst)
        S.add("sp", lambda e: e.dma_start(out=cbf[:], in_=cbf_d), writes=[b_cbf], chan=c_cbf)

        def ring_load(t, pr):
            g = t * NPAIR + pr
            s = g % RS
            i0 = 2 * pr
            if t == 0:
                ensure_conv(47 if pr < CONV_PHASE2_PAIR else None)
            S.add("sp", (lambda e, s=s, i0=i0: e.dma_start(
                out=ring[s][:], in_=wsc_d[i0:i0 + 2].rearrange("i p n -> p i n"))),
                reads=[b_wsc[i0], b_wsc[i0 + 1]], writes=[b_ring[s]], chan=c_ring[s])

        def item_ap(t, it):
            g = t * NPAIR + it // 2
            s = g % RS
            return ring[s][:, it % 2, :], b_ring[s]

        def mm_op(out_ap, pairs, reads, bi):
            def fn(e):
                ins = None
                for (l, r, o, st, sp_) in pairs:
                    ins = e.matmul(o if o is not None else out_ap, l, r, start=st, stop=sp_)
                return ins
            return S.add("pe", fn, reads=reads, writes=[banks[bi]])

        def chain(lrs, first=True, last=True):
            n = len(lrs)
            return [(l, r, None, first and i == 0, last and i == n - 1) for i, (l, r) in enumerate(lrs)]

        def norm_stats(src_ap, b_src, col):
            jt, bjt = nexttmp()
            jv = jt[:].bitcast(BF16)
            S.add("act", lambda e: e.activation(out=jv, in_=src_ap, func=AF.Square, accum_out=ss[:, col:col + 1]),
                  reads=[b_src], writes=[bjt, b_ss[col]])
            S.add("dve", lambda e: e.tensor_scalar(out=rs[:, col:col + 1], in0=ss[:, col:col + 1], scalar1=1.0 / D,
                                                    scalar2=EPS, op0=ALU.mult, op1=ALU.add),
                  reads=[b_ss[col]], writes=[b_rs[col]])
            S.add("pool", lambda e: e.tensor_tensor(out=rs[:, col:col + 1], in0=rs[:, col:col + 1], in1=mhalf[:, 0:1], op=ALU.pow),
                  reads=[b_rs[col], b_small], writes=[b_rs[col]])

        def norm_scale(src_ap, b_src, col):
            k = nextxsb()
            S.add("act", lambda e: e.activation(out=xsb[k][:], in_=src_ap, func=AF.Identity, scale=rs[:, col:col + 1]),
                  reads=[b_src, b_rs[col]], writes=[b_xsb[k]])
            return k

        def norm_transpose(k, gcol, s, dstT, b_dst, extra_w=()):
            bi = nextbank()
            pbf = bank_ap(bi).bitcast(BF16)

            def fn(e):
                ins = None
                for kc in range(8):
                    ins = e.transpose(pbf[:, kc * 128:(kc + 1) * 128], xsb[k][:, kc * 128:(kc + 1) * 128], ident)
                return ins
            S.add("pe", fn, reads=[b_xsb[k], b_cbf], writes=[banks[bi]])
            S.add("dve", lambda e: e.tensor_tensor(
                out=dstT[:, :, s * 128:(s + 1) * 128],
                in0=pbf.rearrange("p (k t) -> p k t", k=8),
                in1=cc(gcol, 8).unsqueeze(2).to_broadcast([128, 8, 128]), op=ALU.mult),
                reads=[banks[bi], b_cst], writes=[b_dst[s]] + list(extra_w))

        n1_k = {}

        n1_io = {}

        def xh_load(t, s):
            r0 = t * T + s * 128
            return S.add("sp", (lambda e: e.dma_start(out=xh[:, s, :], in_=x_d[r0:r0 + 128, :])),
                         writes=[b_xh[s]], chan=c_xh[s])

        def n1_A1(t, s):
            if t == 0:
                src, bsrc = xh[:, s, :], b_xh[s]
                ld = None
            else:
                r0 = t * T + s * 128
                k = nextio()
                ld = S.add("sp", (lambda e, k=k, r0=r0: e.dma_start(out=io[k][:], in_=x_d[r0:r0 + 128, :])),
                           writes=[b_io[k]], chan=c_io_in[k])
                src, bsrc = io[k][:], b_io[k]
            norm_stats(src, bsrc, s)
            n1_io[(t, s)] = (src, bsrc)
            return ld

        def n1_A2(t, s):
            src, bsrc = n1_io[(t, s)]
            n1_k[(t, s)] = norm_scale(src, bsrc, s)

        def n1_B(t, s):
            norm_transpose(n1_k[(t, s)], C_G1, s, nT, b_nT)

        def conv_stream(i0, n, extra=()):
            c = chan("cv%d" % i0)
            S.add("pool", (lambda e: e.dma_start(out=wsc_d[i0:i0 + n].rearrange("i p n -> p i n"),
                                                in_=wst_d[i0:i0 + n].rearrange("i p n -> p i n"))),
                  writes=[b_wsc[i] for i in range(i0, i0 + n)], chan=c, extra=extra)

        def conv_res(i0, n):
            c = chan("cr%d" % i0)
            if i0 < 8:
                dst, bd = wo_r[:, i0:i0 + n, :], b_wo[i0:i0 + n]
            else:
                dst, bd = wd_r[:, i0 - 8:i0 - 8 + n, :], b_wd[i0 - 8:i0 - 8 + n]
            S.add("pool", (lambda e: e.dma_start(out=dst, in_=wres_d[i0:i0 + n].rearrange("i p n -> p i n"))),
                  writes=bd, chan=c)

        S.add("pool", lambda e: e.memset(mhalf[:], -0.5), writes=[b_small])
        S.add("pool", lambda e: e.memset(ones64[:], 1.0), writes=[b_small])
        S.add("pool", lambda e: e.memset(ccx[:, :, 0:2], 0.0), writes=b_ccx)
        S.add("act", lambda e: e.activation(out=sk[:], in_=cc(C_SK, 4), func=AF.Exp), reads=[b_cst], writes=[b_sk])
        CG = 4
        groups = [(i, min(CG, NI - i)) for i in range(0, NI, CG)]
        first_x = [xh_load(0, s) for s in range(NSUB)]
        conv_todo = []
        for (i0, n) in groups:
            if i0 < 44:
                conv_todo.append((i0 + n - 1, (lambda i0=i0, n=n: conv_stream(i0, n, extra=first_x if i0 >= 8 else ()))))
        conv_todo.append((None, lambda: conv_res(0, 8)))
        for (i0, n) in groups:
            if i0 >= 44:
                conv_todo.append((i0 + n - 1, (lambda i0=i0, n=n: conv_stream(i0, n))))
        conv_todo.append((None, lambda: conv_res(8, 8)))
        conv_todo.append((None, lambda: conv_res(16, 8)))
        conv_todo.append((None, lambda: conv_res(24, 6)))
        conv_pos = [0]
        conv_done_item = [-1]
        CONV_PHASE2_PAIR = 12

        def ensure_conv(item):
            while conv_pos[0] < len(conv_todo) and (item is None or conv_done_item[0] < min(item, NI - 1)):
                last, th = conv_todo[conv_pos[0]]
                conv_pos[0] += 1
                th()
                if last is not None:
                    conv_done_item[0] = last

        ensure_conv(7)
        for s in range(NSUB):
            n1_A1(0, s)

        store_ops = []
        for pr in range(RS):
            ring_load(0, pr)
        next_load = [RS]

        def consumed_pair():
            g = next_load[0]
            if g < ntiles * NPAIR:
                ring_load(g // NPAIR, g % NPAIR)
                next_load[0] = g + 1

        for s in range(NSUB):
            n1_A2(0, s)
        for s in range(NSUB):
            n1_B(0, s)

        for t in range(ntiles):
            tok0 = t * T
            if t > 0:
                S.add("pool", lambda e: e.tensor_copy(out=KT[:, 0:128], in_=KT[:, T:T + 128]), reads=[b_KT], writes=[b_KT])
                S.add("pool", lambda e: e.tensor_copy(out=Vt[:, 0, :], in_=Vt[:, 4, :]), reads=[b_V], writes=[b_V])

            def fm_item(it, rhsT=nT, b_rhs=b_nT, split=False):
                ap, bring = item_ap(t, it)
                bi = nextbank()
                if split:
                    p1 = [(ap[:, kc * 128:(kc + 1) * 128], rhsT[:, kc, 0:256], ps[:, bi * 512:bi * 512 + 256], kc == 0, kc == 7)
                          for kc in range(8)]
                    p2 = [(ap[:, kc * 128:(kc + 1) * 128], rhsT[:, kc, 256:512], ps[:, bi * 512 + 256:(bi + 1) * 512], kc == 0, kc == 7)
                          for kc in range(8)]
                    return bi, (lambda: mm_op(None, p1, [bring] + b_hnT[0:2] + hn_al(0) + hn_al(1), bi)), \
                        (lambda: mm_op(None, p2, [bring] + b_hnT[2:4] + hn_al(2) + hn_al(3), bi))
                lrs = [(ap[:, kc * 128:(kc + 1) * 128], rhsT[:, kc, :]) for kc in range(8)]
                mm_op(bank_ap(bi), chain(lrs), [bring] + list(b_rhs), bi)
                if it % 2 == 1:
                    consumed_pair()
                return bi

            it = 0
            for g in range(4):
                bi = fm_item(it)
                bcol = C_BIAS + ITEMS[it][2]
                S.add("act", (lambda e, bi=bi, g=g, bcol=bcol: e.activation(out=QT[:, g, :], in_=bank_ap(bi), func=AF.Identity,
                                                                           bias=cc(bcol), scale=1.0)),
                      reads=[banks[bi], b_cst], writes=[b_QT[g]])
                it += 1
            bi = fm_item(it)
            bcol = C_BIAS + ITEMS[it][2]
            S.add("act", (lambda e, bi=bi, bcol=bcol: e.activation(out=KT[:, 128:128 + T], in_=bank_ap(bi), func=AF.Identity,
                                                                    bias=cc(bcol), scale=1.0)),
                  reads=[banks[bi], b_cst], writes=[b_KT])
            it += 1
            ap, bring = item_ap(t, it)
            bi = nextbank()
            prs = []
            for s in range(NSUB):
                for kc in range(8):
                    prs.append((nT[:, kc, s * 128:(s + 1) * 128], ap[:, kc * 128:(kc + 1) * 128],
                                ps[:, bi * 512 + s * 128: bi * 512 + (s + 1) * 128], kc == 0, kc == 7))
            mm_op(None, prs, [bring] + b_nT, bi)
            consumed_pair()
            S.add("dve", (lambda e, bi=bi: e.tensor_tensor(
                out=Vt[:, 1:5, :], in0=bank_ap(bi).rearrange("p (s n) -> p s n", s=4),
                in1=cc(C_BV, 128).unsqueeze(1).to_broadcast([128, 4, 128]), op=ALU.add)),
                reads=[banks[bi], b_cst], writes=[b_V])
            it += 1

            if t > 0:
                for s in range(NSUB):
                    xh_load(t, s)

            def attn_scores(qb):
                gq = t * 4 + qb
                k = gq % 2
                pcs = [1] if gq == 0 else [0, 1]
                for pc in pcs:
                    for kv in range(2):
                        bi = nextbank()
                        kcol = (qb + pc) * 128
                        lhsT = KT[kv * 64:(kv + 1) * 64, kcol:kcol + 128]
                        rhs = QT[kv * 64:(kv + 1) * 64, :, qb * 128:(qb + 1) * 128]
                        mm_op(bank_ap(bi), [(lhsT, rhs, None, True, True)], [b_KT] + b_QT, bi)
                        idx = pc * 2 + kv
                        S.add("act", (lambda e, bi=bi, k=k, idx=idx: e.activation(
                            out=PT[k][:, idx * 512:(idx + 1) * 512], in_=bank_ap(bi), func=AF.Exp, scale=0.125)),
                            reads=[banks[bi]], writes=[b_PT[k][pc][kv]])
                for pc in pcs:
                    m = maskp if pc == 0 else maskc
                    S.add("dve" if pc == 0 else "pool", (lambda e, k=k, pc=pc, m=m: e.tensor_tensor(
                        out=PT[k][:, pc * 1024:(pc + 1) * 1024].rearrange("p (h q) -> p h q", h=8),
                        in0=PT[k][:, pc * 1024:(pc + 1) * 1024].rearrange("p (h q) -> p h q", h=8),
                        in1=m.unsqueeze(1).to_broadcast([128, 8, 128]), op=ALU.mult)),
                        reads=[b_PT[k][pc][0], b_PT[k][pc][1], b_cbf], writes=[b_PT[k][pc][0], b_PT[k][pc][1]])

            def attn_pv(qb):
                gq = t * 4 + qb
                k = gq % 2
                pcs = [1] if gq == 0 else [0, 1]
                rd = [b_V, b_small] + [b_PT[k][pc][kv] for pc in pcs for kv in range(2)]
                bu = nextbank()
                prs = []
                for kv in range(2):
                    for n_, pc in enumerate(pcs):
                        idx = pc * 2 + kv
                        prs.append((Vt[:, qb + pc, kv * 64:(kv + 1) * 64], PT[k][:, idx * 512:(idx + 1) * 512],
                                    ps[kv * 64:(kv + 1) * 64, bu * 512:(bu + 1) * 512], n_ == 0, n_ == len(pcs) - 1))
                mm_op(None, prs, rd, bu)
                bd = nextbank()
                prs = []
                for kv in range(2):
                    for n_, pc in enumerate(pcs):
                        idx = pc * 2 + kv
                        prs.append((ones64[:, :], PT[k][:, idx * 512:(idx + 1) * 512],
                                    ps[kv * 64:(kv + 1) * 64, bd * 512:(bd + 1) * 512], n_ == 0, n_ == len(pcs) - 1))
                mm_op(None, prs, rd, bd)
                tp, btp = nexttmp()

                def lnfn(e, bd=bd, tp=tp):
                    ins = None
                    for g in range(4):
                        ins = e.activation(out=tp[:, g * 128:(g + 1) * 128], in_=bank_ap(bd)[:, g * 128:(g + 1) * 128],
                                           func=AF.Ln, bias=sk[:, g:g + 1], scale=1.0)
                    return ins
                S.add("act", lnfn, reads=[banks[bd], b_sk], writes=[btp])
                S.add("act", (lambda e, tp=tp: e.activation(out=tp[:], in_=tp[:], func=AF.Exp, scale=-1.0)),
                      reads=[btp], writes=[btp])
                S.add("dve", (lambda e, bu=bu, tp=tp, qb=qb: e.tensor_tensor(
                    out=attnT[:, :, qb * 128:(qb + 1) * 128], in0=bank_ap(bu).rearrange("p (g q) -> p g q", g=4),
                    in1=tp[:].rearrange("p (g q) -> p g q", g=4), op=ALU.mult)),
                    reads=[banks[bu], btp], writes=[b_attnT[qb]])

            def conv_chunk(c):
                nonlocal it
                bx = fm_item(it)
                bcol = C_BIAS + ITEMS[it][2]
                t1, bt1 = nexttmp()
                S.add("act", (lambda e, bx=bx, t1=t1, bcol=bcol: e.activation(out=t1[:], in_=bank_ap(bx), func=AF.Identity,
                                                                             bias=cc(bcol), scale=1.0)),
                      reads=[banks[bx], b_cst], writes=[bt1])
                it += 1
                bc_ = fm_item(it)
                bcol = C_BIAS + ITEMS[it][2]
                S.add("dve", (lambda e, bc_=bc_, t1=t1, bcol=bcol, c=c: e.scalar_tensor_tensor(
                    out=ccx[:, c, 2:T + 2], in0=bank_ap(bc_), scalar=cc(bcol), in1=t1[:], op0=ALU.add, op1=ALU.mult)),
                    reads=[banks[bc_], b_cst, bt1], writes=[b_ccx[c]])
                it += 1
                a, ba = nexttmp()
                S.add("pool", (lambda e, a=a, c=c: e.tensor_scalar(out=a[:], in0=ccx[:, c, 2:T + 2], scalar1=cc(C_CW + c * 3 + 2),
                                                                 scalar2=0.0, op0=ALU.mult, op1=ALU.add)),
                      reads=[b_ccx[c], b_cst], writes=[ba])
                for kk in (1, 0):
                    S.add("dve", (lambda e, a=a, c=c, kk=kk: e.scalar_tensor_tensor(
                        out=a[:], in0=ccx[:, c, kk:kk + T], scalar=cc(C_CW + c * 3 + kk), in1=a[:], op0=ALU.mult, op1=ALU.add)),
                        reads=[b_ccx[c], b_cst, ba], writes=[ba])
                S.add("pool", (lambda e, c=c: e.tensor_copy(out=ccx[:, c, 0:2], in_=ccx[:, c, T:T + 2])),
                      reads=[b_ccx[c]], writes=[b_ccx[c]])
                bb = fm_item(it)
                bcol = C_BIAS + ITEMS[it][2]
                S.add("dve", (lambda e, bb=bb, a=a, bcol=bcol, c=c: e.scalar_tensor_tensor(
                    out=convT[:, c, :], in0=bank_ap(bb), scalar=cc(bcol), in1=a[:], op0=ALU.add, op1=ALU.mult)),
                    reads=[banks[bb], b_cst, ba], writes=[b_convT[c]] + b_ctok)
                it += 1

            attn_scores(0)
            for c in range(4):
                conv_chunk(c)
                if c + 1 < 4:
                    attn_scores(c + 1)
                attn_pv(c)

            for c in range(8):
                bA = fm_item(it)
                bcolA = C_BIAS + ITEMS[it][2]
                it += 1
                bB = fm_item(it)
                bcolB = C_BIAS + ITEMS[it][2]
                it += 1
                ap, bring = item_ap(t, it)
                bC = nextbank()
                mm_op(bank_ap(bC), chain([(ap[:, g * 128:(g + 1) * 128], attnT[:, g, :]) for g in range(4)]),
                      [bring] + b_attnT, bC)
                bD = nextbank()
                mm_op(bank_ap(bD), chain([(ap[:, (4 + g) * 128:(5 + g) * 128], convT[:, g, :]) for g in range(4)]),
                      [bring] + b_convT + b_ctok, bD)
                if it % 2 == 1:
                    consumed_pair()
                it += 1
                t1, bt1 = nexttmp()
                t2, bt2 = nexttmp()
                S.add("act", (lambda e, bA=bA, t1=t1, bcolA=bcolA: e.activation(out=t1[:], in_=bank_ap(bA), func=AF.Sigmoid,
                                                                               bias=cc(bcolA), scale=1.0)),
                      reads=[banks[bA], b_cst], writes=[bt1])
                S.add("act", (lambda e, bB=bB, t2=t2, bcolB=bcolB: e.activation(out=t2[:], in_=bank_ap(bB), func=AF.Sigmoid,
                                                                               bias=cc(bcolB), scale=1.0)),
                      reads=[banks[bB], b_cst], writes=[bt2])
                S.add("dve", (lambda e, bC=bC, t1=t1: e.tensor_tensor(out=t1[:], in0=bank_ap(bC), in1=t1[:], op=ALU.mult)),
                      reads=[banks[bC], bt1], writes=[bt1])
                S.add("dve", (lambda e, bD=bD, t2=t2: e.tensor_tensor(out=t2[:], in0=bank_ap(bD), in1=t2[:], op=ALU.mult)),
                      reads=[banks[bD], bt2], writes=[bt2])
                S.add("pool", (lambda e, t1=t1, t2=t2, c=c: e.tensor_tensor(out=mergedT[:, c, :], in0=t1[:], in1=t2[:], op=ALU.add)),
                      reads=[bt1, bt2], writes=[b_merged[c]])
            assert it == NI_MIX

            if t == 0:
                ensure_conv(47)
            hk = {}

            def wout_mm(s):
                bis_ = [nextbank(), nextbank()]
                lr = [[(mergedT[:, kc, s * 128:(s + 1) * 128], wo_r[:, kc, hh * 512:(hh + 1) * 512]) for kc in range(8)]
                      for hh in range(2)]
                if s == 0:
                    for (a0, a1) in ((0, 6), (6, 7), (7, 8)):
                        for hh in range(2):
                            mm_op(bank_ap(bis_[hh]), chain(lr[hh][a0:a1], first=(a0 == 0), last=(a1 == 8)),
                                  b_merged[a0:a1] + b_wo[a0:a1], bis_[hh])
                else:
                    for hh in range(2):
                        mm_op(bank_ap(bis_[hh]), chain(lr[hh]), b_merged + b_wo, bis_[hh])
                for hh in range(2):
                    bi = bis_[hh]
                    S.add("dve", (lambda e, bi=bi, s=s, hh=hh: e.tensor_tensor(
                        out=xh[:, s, hh * 512:(hh + 1) * 512], in0=bank_ap(bi), in1=xh[:, s, hh * 512:(hh + 1) * 512], op=ALU.add)),
                        reads=[banks[bi], b_xh[s]], writes=[b_xh[s]])

            def n2_stats(s):
                norm_stats(xh[:, s, :], b_xh[s], 4 + s)

            def n2_scale(s):
                hk[s] = norm_scale(xh[:, s, :], b_xh[s], 4 + s)

            def n2_T(s):
                norm_transpose(hk[s], C_G2, s, hnT, b_hnT, extra_w=hn_al(s))

            wout_mm(0); n2_stats(0)
            wout_mm(1); n2_stats(1); n2_scale(0)
            wout_mm(2); n2_stats(2); n2_scale(1)
            n2_T(0)
            wout_mm(3)
            n2_scale(2)
            n2_T(1)
            n2_stats(3)
            n2_scale(3)

            if t > 0:
                def fwv(k):
                    return cst[:, C_FW:C_FW + 132].rearrange("p (m k) -> p m k", k=3)[:, :, k]
                S.add("pool", lambda e: e.tensor_tensor(out=fx[:, :, 0], in0=uh[:, :, 0], in1=fwv(0), op=ALU.mult),
                      reads=[b_uh, b_cst], writes=[b_fx])
                S.add("pool", lambda e: e.tensor_tensor(out=fxt[:], in0=uh[:, :, 1], in1=fwv(1), op=ALU.mult),
                      reads=[b_uh, b_cst], writes=[b_fxt])
                S.add("pool", lambda e: e.tensor_tensor(out=fx[:, :, 0], in0=fx[:, :, 0], in1=fxt[:], op=ALU.add),
                      reads=[b_fx, b_fxt], writes=[b_fx])
                S.add("pool", lambda e: e.tensor_tensor(out=fx[:, :, 1], in0=uh[:, :, 1], in1=fwv(0), op=ALU.mult),
                      reads=[b_uh, b_cst], writes=[b_fx])

            hooks = {}
            if t + 1 < ntiles:
                for s in range(NSUB):
                    hooks.setdefault(1 + 2 * s, []).append(lambda s=s: n1_A1(t + 1, s))
                    hooks.setdefault(3 + 2 * s, []).append(lambda s=s: n1_A2(t + 1, s))
                    hooks.setdefault(6 + 2 * s, []).append(lambda s=s: n1_B(t + 1, s))

            pending = None
            for j in range(22):
                accs = []
                bis = []
                if j < 2 and False:
                    pass
                if j == 0:
                    late = []
                    split_bis = []
                    n2_T(2)
                    for q in range(4):
                        bi_, f1, f2 = fm_item(it + q, hnT, b_hnT + hn_alias, split=True)
                        f1()
                        late.append(f2)
                        split_bis.append(bi_)
                        if q == 1:
                            n2_T(3)
                    for q, f2 in enumerate(late):
                        f2()
                        if q % 2 == 1:
                            consumed_pair()
                if j < 2:
                    for half in range(2):
                        bis.append(split_bis[2 * j + half])
                        it += 1
                else:
                    for half in range(2):
                        bis.append(fm_item(it, hnT, b_hnT + hn_alias))
                        it += 1
                for half in range(2):
                    m = 2 * j + half
                    bi = bis[half]
                    a, ba = nexttmp()
                    fcol = C_FW + m * 3
                    S.add("act", (lambda e, bi=bi, a=a, fcol=fcol: e.activation(out=a[:], in_=bank_ap(bi), func=AF.Identity,
                                                                               scale=cc(fcol + 2))),
                          reads=[banks[bi], b_cst], writes=[ba])
                    if t + 1 < ntiles:
                        S.add("act", (lambda e, bi=bi, m=m: e.activation(out=uh[:, m, :], in_=bank_ap(bi)[:, T - 2:T], func=AF.Copy)),
                              reads=[banks[bi]], writes=[b_uh])
                    if t > 0:
                        S.add("pool", (lambda e, a=a, m=m: e.tensor_tensor(out=a[:, 0:2], in0=a[:, 0:2], in1=fx[:, m, :], op=ALU.add)),
                              reads=[ba, b_fx], writes=[ba])
                    accs.append((a, ba, bi, fcol))
                for (kk, lo) in ((1, 1), (0, 2)):
                    for (a, ba, bi, fcol) in accs:
                        S.add("dve", (lambda e, bi=bi, a=a, fcol=fcol, kk=kk, lo=lo: e.scalar_tensor_tensor(
                            out=a[:, lo:T], in0=bank_ap(bi)[:, 0:T - lo], scalar=cc(fcol + kk), in1=a[:, lo:T],
                            op0=ALU.mult, op1=ALU.add)),
                            reads=[banks[bi], b_cst, ba], writes=[ba])
                (ag, bag, _, _), (av, bav, _, _) = accs

                def finish(ag=ag, bag=bag, av=av, bav=bav, j=j):
                    S.add("act", (lambda e: e.activation(out=ag[:], in_=ag[:], func=AF.Silu)), reads=[bag], writes=[bag])
                    S.add("pool", (lambda e: e.tensor_tensor(out=actT[:, j, :], in0=ag[:], in1=av[:], op=ALU.mult)),
                          reads=[bag, bav], writes=[b_actT[j]])
                if pending is not None:
                    pending()
                pending = finish
                for h in hooks.get(j, []):
                    h()
            pending()
            assert it == NI
            if t + 1 < ntiles:
                S.add("act", lambda e: e.activation(out=dummy[:, 0:1], in_=mhalf[:, 0:1], func=AF.Exp),
                      reads=[b_small], writes=[b_dummy])
            if t == 0:
                ensure_conv(None)

            for s in range(NSUB):
                for hh in range(2):
                    bi = nextbank()
                    lrs = [(actT[:, j, s * 128:(s + 1) * 128], wd_r[:, j, hh * 512:(hh + 1) * 512]) for j in range(22)]
                    if s == 0 and hh == 0:
                        cuts = [0, 12, 16, 19, 21, 22]
                        for ci in range(len(cuts) - 1):
                            a0, a1 = cuts[ci], cuts[ci + 1]
                            mm_op(bank_ap(bi), chain(lrs[a0:a1], first=(a0 == 0), last=(a1 == 22)),
                                  b_actT[a0:a1] + b_wd[a0:a1], bi)
                    else:
                        mm_op(bank_ap(bi), chain(lrs), b_actT + b_wd, bi)
                    S.add("dve", (lambda e, bi=bi, s=s, hh=hh: e.tensor_tensor(
                        out=xh[:, s, hh * 512:(hh + 1) * 512], in0=bank_ap(bi), in1=xh[:, s, hh * 512:(hh + 1) * 512], op=ALU.add)),
                        reads=[banks[bi], b_xh[s]], writes=[b_xh[s]])
                norm_stats(xh[:, s, :], b_xh[s], 8 + s)
                k = nextio()
                S.add("dve", (lambda e, k=k, s=s: e.scalar_tensor_tensor(
                    out=io[k][:], in0=xh[:, s, :], scalar=rs[:, 8 + s:9 + s], in1=cc(C_GF, 1024), op0=ALU.mult, op1=ALU.mult)),
                    reads=[b_xh[s], b_rs[8 + s], b_cst], writes=[b_io[k]])
                r0 = tok0 + s * 128
                store_ops.append(S.add("sp", (lambda e, k=k, r0=r0: e.dma_start(out=out_d[r0:r0 + 128, :], in_=io[k][:])),
                                       reads=[b_io[k]], chan=c_io_out[k]))

        S.add("sp", lambda e: e.nop(), extra=store_ops)
        S.run(nc, sems)
    return nc


def _fm_item(Wcols):
    return np.ascontiguousarray(Wcols.reshape(8, 128, 128).transpose(1, 0, 2).reshape(128, 1024))


def prepare_weights(mix_norm, w_in, b_in, sinks, conv_w, w_attn_branch, w_conv_branch, w_out,
                    ffn_norm, w_up, ffn_conv_w, w_down, final_norm):
    f32 = np.float32
    w_in = np.asarray(w_in, f32)[0]; b_in = np.asarray(b_in, f32)[0]
    wa = np.asarray(w_attn_branch, f32)[0]; wc = np.asarray(w_conv_branch, f32)[0]
    wo = np.asarray(w_out, f32)[0]; wu = np.asarray(w_up, f32)[0]; wd = np.asarray(w_down, f32)[0]
    cw = np.asarray(conv_w, f32)[0]; fcw = np.asarray(ffn_conv_w, f32)[0]
    sinks = np.asarray(sinks, f32)[0]
    ar = np.arange(128)
    kvd = np.array([kv * 256 + d for kv in range(2) for d in range(64)])

    def cols_of(kind, idx):
        if kind == "q":
            return kvd + idx * 64
        if kind == "k":
            return 512 + ar
        if kind == "v":
            return 640 + ar
        if kind == "cb":
            return 768 + idx * 128 + ar
        if kind == "cc":
            return 1280 + idx * 128 + ar
        if kind == "cx":
            return 1792 + idx * 128 + ar
        if kind == "ga":
            return 2304 + idx * 128 + ar
        if kind == "gc":
            return 3328 + idx * 128 + ar
        raise ValueError(kind)

    wst = np.zeros((NI, 128, 1024), f32)
    cst = np.zeros((128, NCST), f32)
    for i, (kind, idx, bcol) in enumerate(ITEMS):
        if kind == "wac":
            c = idx
            for g in range(4):
                wst[i][:, g * 128:(g + 1) * 128] = wa[kvd + g * 64, c * 128:(c + 1) * 128]
                wst[i][:, (4 + g) * 128:(5 + g) * 128] = wc[g * 128 + ar, c * 128:(c + 1) * 128]
        elif kind == "fg":
            wst[i] = _fm_item(wu[:, idx * 128:(idx + 1) * 128])
        elif kind == "fv":
            wst[i] = _fm_item(wu[:, 2816 + idx * 128:2816 + (idx + 1) * 128])
        else:
            cols = cols_of(kind, idx)
            wst[i] = _fm_item(w_in[:, cols])
            cst[:, C_BIAS + bcol] = b_in[cols]
    wres = np.zeros((NRES, 128, 1024), f32)
    for kc in range(8):
        wres[kc] = wo[kc * 128:(kc + 1) * 128, :]
    for j in range(22):
        wres[8 + j] = wd[j * 128:(j + 1) * 128, :]
    cst[:, C_G1:C_G1 + 8] = np.asarray(mix_norm, f32)[0].reshape(8, 128).T
    cst[:, C_G2:C_G2 + 8] = np.asarray(ffn_norm, f32)[0].reshape(8, 128).T
    for c in range(4):
        for k in range(3):
            cst[:, C_CW + c * 3 + k] = cw[k, c * 128:(c + 1) * 128]
    for j in range(22):
        for half in range(2):
            m = 2 * j + half
            base = j * 128 if half == 0 else 2816 + j * 128
            for k in range(3):
                cst[:, C_FW + m * 3 + k] = fcw[k, base:base + 128]
    for p in range(128):
        cst[p, C_SK:C_SK + 4] = sinks[(p // 64) * 4:(p // 64) * 4 + 4]
    cst[:, C_BV:C_BV + 128] = b_in[640:768][None, :]
    cst[:, C_GF:C_GF + 1024] = np.asarray(final_norm, f32)[None, :]
    bf = ml_dtypes.bfloat16
    cbf = np.zeros((128, 384), bf)
    cbf[:, 0:128] = np.eye(128).astype(bf)
    kk = np.arange(128)[:, None]; qq = np.arange(128)[None, :]
    cbf[:, 128:256] = (kk > qq).astype(bf)
    cbf[:, 256:384] = (kk <= qq).astype(bf)
    return wst, wres, cst, cbf


_NC_CACHE = {}


def run(x, weights, ntiles):
    wst, wres, cst, cbf = weights
    B = x.shape[0]
    if ntiles not in _NC_CACHE:
        _NC_CACHE[ntiles] = build_nc(ntiles)
    nc = _NC_CACHE[ntiles]
    in_maps = [{"x": np.ascontiguousarray(x[b]), "wst": wst, "wres": wres, "cst": cst, "cbf": cbf} for b in range(B)]
    res = run_bass_kernel_spmd(nc, in_maps, core_ids=list(range(B)))
    return np.stack([res.results[b]["out"] for b in range(B)], axis=0)


def kernel(x, mix_norm, w_in, b_in, sinks, conv_w, w_attn_branch, w_conv_branch, w_out,
           ffn_norm, w_up, ffn_conv_w, w_down, final_norm):
    x = np.asarray(x, np.float32)
    weights = prepare_weights(mix_norm, w_in, b_in, sinks, conv_w, w_attn_branch, w_conv_branch, w_out,
                              ffn_norm, w_up, ffn_conv_w, w_down, final_norm)
    assert x.shape[1] % T == 0
    return run(x, weights, x.shape[1] // T).astype(np.float32)
```

```python
import numpy as np
import ml_dtypes
from contextlib import ExitStack
import concourse.bass as bass
import concourse.mybir as mybir
from concourse.bass_utils import run_bass_kernel_spmd

F32 = mybir.dt.float32
BF16 = mybir.dt.bfloat16
AF = mybir.ActivationFunctionType
ALU = mybir.AluOpType

D = 1024
T = 512
NSUB = 4
NI_MIX = 42
NI_FFN = 44
NI = NI_MIX + NI_FFN
NPAIR = NI // 2
NRES = 30
RS = 4
NTMP = 8
NXSB = 4
EPS = 1e-5

C_BIAS = 0
C_G1 = 34
C_G2 = 42
C_CW = 50
C_FW = 62
C_SK = 194
C_BV = 198
C_GF = 326
NCST = 1350


class Buf:
    def __init__(self, name, exclusive=False):
        self.name = name
        self.exclusive = exclusive
        self.last_w = None
        self.readers = []


class Chan:
    def __init__(self, name):
        self.name = name
        self.sem = None
        self.count = 0


class Op:
    __slots__ = ("eng", "fn", "deps", "signal", "sigval", "chan", "chanval", "idx")

    def __init__(self, eng, fn, chan=None):
        self.eng = eng
        self.fn = fn
        self.deps = []
        self.signal = False
        self.sigval = 0
        self.chan = chan
        self.chanval = 0
        self.idx = 0


ENGINES = ("pe", "act", "dve", "pool", "sp")
NOSELF = ("pe", "sp")


class Sched:
    def __init__(self):
        self.ops = {e: [] for e in ENGINES}
        self.chans = []

    def chan(self, name):
        c = Chan(name)
        self.chans.append(c)
        return c

    def add(self, eng, fn, reads=(), writes=(), chan=None, extra=()):
        op = Op(eng, fn, chan)
        deps = {}

        def need(o):
            if o is None or o is op:
                return
            deps[id(o)] = o

        for b in reads:
            need(b.last_w)
            if b.exclusive:
                for r in b.readers:
                    if r.eng != eng:
                        need(r)
        for b in writes:
            need(b.last_w)
            for r in b.readers:
                need(r)
        for o in extra:
            need(o)
        for b in reads:
            b.readers.append(op)
        for b in writes:
            b.last_w = op
            b.readers = []
        if chan is not None:
            chan.count += 1
            op.chanval = 16 * chan.count
        op.idx = len(self.ops[eng])
        best = {}
        out = []
        for o in deps.values():
            if o.chan is not None:
                out.append(o)
                continue
            if o.eng == eng and eng in NOSELF:
                continue
            cur = best.get(o.eng)
            if cur is None or o.idx > cur.idx:
                best[o.eng] = o
        out.extend(best.values())
        op.deps = out
        self.ops[eng].append(op)
        return op

    def run(self, nc, sems):
        for e in ENGINES:
            for op in self.ops[e]:
                for d in op.deps:
                    if d.chan is None:
                        d.signal = True
        for e in ENGINES:
            n = 0
            for op in self.ops[e]:
                if op.chan is None and op.signal:
                    n += 1
                    op.sigval = n

        def replay(name, e):
            waited = {}
            for op in self.ops[name]:
                for d in op.deps:
                    if d.chan is not None:
                        key, sem, val = ("c", id(d.chan)), d.chan.sem, d.chanval
                    else:
                        key, sem, val = ("e", d.eng), sems[d.eng], d.sigval
                    if waited.get(key, 0) >= val:
                        continue
                    e.wait_ge(sem, val)
                    waited[key] = val
                ins = op.fn(e)
                if op.chan is not None:
                    ins.then_inc(op.chan.sem, 16)
                elif op.signal:
                    ins.then_inc(sems[name], 1)

        with nc.Block() as block:
            @block.tensor
            def _(e):
                replay("pe", e)

            @block.scalar
            def _(e):
                replay("act", e)

            @block.vector
            def _(e):
                replay("dve", e)

            @block.gpsimd
            def _(e):
                replay("pool", e)

            @block.sync
            def _(e):
                replay("sp", e)


def item_table():
    items = []
    nb = 0
    for g in range(4):
        items.append(("q", g, nb)); nb += 1
    items.append(("k", 0, nb)); nb += 1
    items.append(("v", 0, nb)); nb += 1
    for c in range(4):
        for kind in ("cx", "cc", "cb"):
            items.append((kind, c, nb)); nb += 1
    for c in range(8):
        items.append(("ga", c, nb)); nb += 1
        items.append(("gc", c, nb)); nb += 1
        items.append(("wac", c, -1))
    assert nb == 34 and len(items) == NI_MIX
    for j in range(22):
        items.append(("fg", j, -1))
        items.append(("fv", j, -1))
    assert len(items) == NI
    return items


ITEMS = item_table()


def build_nc(ntiles):
    S_TOK = ntiles * T
    nc = bass.Bass("TRN2", target_bir_lowering=False)
    x_d = nc.dram_tensor("x", [S_TOK, D], F32, kind="ExternalInput").ap()
    wst_d = nc.dram_tensor("wst", [NI, 128, 1024], F32, kind="ExternalInput").ap()
    wres_d = nc.dram_tensor("wres", [NRES, 128, 1024], F32, kind="ExternalInput").ap()
    cst_d = nc.dram_tensor("cst", [128, NCST], F32, kind="ExternalInput").ap()
    cbf_d = nc.dram_tensor("cbf", [128, 384], BF16, kind="ExternalInput").ap()
    out_d = nc.dram_tensor("out", [S_TOK, D], F32, kind="ExternalOutput").ap()
    wsc_d = nc.dram_tensor("wsc", [NI, 128, 1024], BF16).ap()

    S = Sched()
    with ExitStack() as ctx:
        def sb(name, shape, dt):
            return ctx.enter_context(nc.sbuf_tensor("s_" + name, shape, dt))

        def newsem(name):
            return ctx.enter_context(nc.semaphore(name))

        def chan(name):
            c = S.chan(name)
            c.sem = newsem("c_" + name)
            return c

        cst = sb("cst", [128, NCST], F32)
        cbf = sb("cbf", [128, 384], BF16)
        wo_r = sb("wo_r", [128, 8, 1024], BF16)
        wd_r = sb("wd_r", [128, 22, 1024], BF16)
        ring = [sb("ring%d" % i, [128, 2, 1024], BF16) for i in range(RS)]
        io = [sb("io%d" % i, [128, 1024], F32) for i in range(3)]
        xsb = [sb("xsb%d" % i, [128, 1024], BF16) for i in range(NXSB)]
        nT = sb("nT", [128, 8, T], BF16)
        QT = sb("QT", [128, 4, T], BF16)
        KT = sb("KT", [128, 128 + T], BF16)
        Vt = sb("Vt", [128, 5, 128], BF16)
        ccx = sb("ccx", [128, 4, T + 2], F32)
        AC = sb("AC", [128, 8, T], BF16)
        PT = [sb("PT%d" % i, [128, 4 * 512], BF16) for i in range(2)]
        mergedT = sb("mergedT", [128, 8, T], BF16)
        xh = sb("xh", [128, NSUB, D], F32)
        tmps = [sb("tmp%d" % i, [128, 512], F32) for i in range(NTMP)]
        actT = sb("actT", [128, 22, T], BF16)
        uh = sb("uh", [128, 44, 2], F32)
        fx = sb("fx", [128, 44, 2], F32)
        fxt = sb("fxt", [128, 44], F32)
        ss = sb("ss", [128, 16], F32)
        rs = sb("rs", [128, 16], F32)
        sk = sb("sk", [128, 4], F32)
        mhalf = sb("mhalf", [128, 1], F32)
        dummy = sb("dummy", [128, 2], F32)
        ones64 = sb("ones64", [128, 64], BF16)
        ps = ctx.enter_context(nc.psum_tensor("ps", [128, 8 * 512], F32))
        attnT = AC[:, 0:4, :]
        convT = AC[:, 4:8, :]
        hnT = AC

        sems = {e: newsem("sem_" + e) for e in ENGINES}

        b_cst = Buf("cst"); b_cbf = Buf("cbf")
        b_wo = [Buf("wo%d" % i) for i in range(8)]
        b_wd = [Buf("wd%d" % i) for i in range(22)]
        b_ring = [Buf("ring%d" % i) for i in range(RS)]
        c_ring = [chan("ring%d" % i) for i in range(RS)]
        b_io = [Buf("io%d" % i) for i in range(3)]
        c_io_in = [chan("ioin%d" % i) for i in range(3)]
        c_io_out = [chan("ioout%d" % i) for i in range(3)]
        b_xsb = [Buf("xsb%d" % i) for i in range(NXSB)]
        b_nT = [Buf("nT%d" % i) for i in range(NSUB)]
        b_hnT = [Buf("hnT%d" % i) for i in range(NSUB)]
        b_QT = [Buf("QT%d" % i) for i in range(4)]
        b_KT = Buf("KT"); b_V = Buf("V")
        b_ccx = [Buf("ccx%d" % i) for i in range(4)]
        b_convT = [Buf("convT%d" % i) for i in range(4)]
        b_PT = [[[Buf("PT%d_%d_%d" % (i, p, k)) for k in range(2)] for p in range(2)] for i in range(2)]
        b_attnT = [Buf("attnT%d" % i) for i in range(4)]
        b_merged = [Buf("merged%d" % i) for i in range(8)]
        b_xh = [Buf("xh%d" % i) for i in range(NSUB)]
        c_xh = [chan("xh%d" % i) for i in range(NSUB)]
        b_tmp = [Buf("tmp%d" % i) for i in range(NTMP)]
        b_actT = [Buf("actT%d" % i) for i in range(22)]
        b_uh = Buf("uh"); b_fx = Buf("fx"); b_fxt = Buf("fxt")
        b_ss = [Buf("ss%d" % i) for i in range(16)]
        b_rs = [Buf("rs%d" % i) for i in range(16)]
        b_sk = Buf("sk"); b_small = Buf("small"); b_dummy = Buf("dummy")
        banks = [Buf("bank%d" % i, exclusive=True) for i in range(8)]
        b_wsc = [Buf("wsc%d" % i) for i in range(NI)]
        c_cst = chan("cst")
        c_cbf = chan("cbf")
        b_ctok = [Buf("ctok%d" % i) for i in range(NSUB)]
        hn_alias = b_attnT + b_ctok

        def hn_al(s):
            return [b_attnT[s], b_ctok[s]]

        state = {"bank": 0, "tmp": 0, "io": 0, "xsb": 0}

        def nextbank():
            i = state["bank"]
            state["bank"] = (i + 1) % 8
            return i

        def bank_ap(i):
            return ps[:, i * 512:(i + 1) * 512]

        def nexttmp():
            i = state["tmp"]
            state["tmp"] = (i + 1) % NTMP
            return tmps[i], b_tmp[i]

        def nextio():
            i = state["io"]
            state["io"] = (i + 1) % 3
            return i

        def nextxsb():
            i = state["xsb"]
            state["xsb"] = (i + 1) % NXSB
            return i

        def cc(col, n=1):
            return cst[:, col:col + n]

        ident = cbf[:, 0:128]
        maskp = cbf[:, 128:256]
        maskc = cbf[:, 256:384]

        S.add("sp", lambda e: e.dma_start(out=cst[:], in_=cst_d), writes=[b_cst], chan=c_cst)
        S.add("sp", lambda e: e.dma_start(out=cbf[:], in_=cbf_d), writes=[b_cbf], chan=c_cbf)

        def ring_load(t, pr):
            g = t * NPAIR + pr
            s = g % RS
            i0 = 2 * pr
            if t == 0:
                ensure_conv(47 if pr < CONV_PHASE2_PAIR else None)
            S.add("sp", (lambda e, s=s, i0=i0: e.dma_start(
                out=ring[s][:], in_=wsc_d[i0:i0 + 2].rearrange("i p n -> p i n"))),
                reads=[b_wsc[i0], b_wsc[i0 + 1]], writes=[b_ring[s]], chan=c_ring[s])

        def item_ap(t, it):
            g = t * NPAIR + it // 2
            s = g % RS
            return ring[s][:, it % 2, :], b_ring[s]

        def mm_op(out_ap, pairs, reads, bi):
            def fn(e):
                ins = None
                for (l, r, o, st, sp_) in pairs:
                    ins = e.matmul(o if o is not None else out_ap, l, r, start=st, stop=sp_)
                return ins
            return S.add("pe", fn, reads=reads, writes=[banks[bi]])

        def chain(lrs, first=True, last=True):
            n = len(lrs)
            return [(l, r, None, first and i == 0, last and i == n - 1) for i, (l, r) in enumerate(lrs)]

        def norm_stats(src_ap, b_src, col):
            jt, bjt = nexttmp()
            jv = jt[:].bitcast(BF16)
            S.add("act", lambda e: e.activation(out=jv, in_=src_ap, func=AF.Square, accum_out=ss[:, col:col + 1]),
                  reads=[b_src], writes=[bjt, b_ss[col]])
            S.add("dve", lambda e: e.tensor_scalar(out=rs[:, col:col + 1], in0=ss[:, col:col + 1], scalar1=1.0 / D,
                                                    scalar2=EPS, op0=ALU.mult, op1=ALU.add),
                  reads=[b_ss[col]], writes=[b_rs[col]])
            S.add("pool", lambda e: e.tensor_tensor(out=rs[:, col:col + 1], in0=rs[:, col:col + 1], in1=mhalf[:, 0:1], op=ALU.pow),
                  reads=[b_rs[col], b_small], writes=[b_rs[col]])

        def norm_scale(src_ap, b_src, col):
            k = nextxsb()
            S.add("act", lambda e: e.activation(out=xsb[k][:], in_=src_ap, func=AF.Identity, scale=rs[:, col:col + 1]),
                  reads=[b_src, b_rs[col]], writes=[b_xsb[k]])
            return k

        def norm_transpose(k, gcol, s, dstT, b_dst, extra_w=()):
            bi = nextbank()
            pbf = bank_ap(bi).bitcast(BF16)

            def fn(e):
                ins = None
                for kc in range(8):
                    ins = e.transpose(pbf[:, kc * 128:(kc + 1) * 128], xsb[k][:, kc * 128:(kc + 1) * 128], ident)
                return ins
            S.add("pe", fn, reads=[b_xsb[k], b_cbf], writes=[banks[bi]])
            S.add("dve", lambda e: e.tensor_tensor(
                out=dstT[:, :, s * 128:(s + 1) * 128],
                in0=pbf.rearrange("p (k t) -> p k t", k=8),
                in1=cc(gcol, 8).unsqueeze(2).to_broadcast([128, 8, 128]), op=ALU.mult),
                reads=[banks[bi], b_cst], writes=[b_dst[s]] + list(extra_w))

        n1_k = {}

        n1_io = {}

        def xh_load(t, s):
            r0 = t * T + s * 128
            return S.add("sp", (lambda e: e.dma_start(out=xh[:, s, :], in_=x_d[r0:r0 + 128, :])),
                         writes=[b_xh[s]], chan=c_xh[s])

        def n1_A1(t, s):
            if t == 0:
                src, bsrc = xh[:, s, :], b_xh[s]
                ld = None
            else:
                r0 = t * T + s * 128
                k = nextio()
                ld = S.add("sp", (lambda e, k=k, r0=r0: e.dma_start(out=io[k][:], in_=x_d[r0:r0 + 128, :])),
                           writes=[b_io[k]], chan=c_io_in[k])
                src, bsrc = io[k][:], b_io[k]
            norm_stats(src, bsrc, s)
            n1_io[(t, s)] = (src, bsrc)
            return ld

        def n1_A2(t, s):
            src, bsrc = n1_io[(t, s)]
            n1_k[(t, s)] = norm_scale(src, bsrc, s)

        def n1_B(t, s):
            norm_transpose(n1_k[(t, s)], C_G1, s, nT, b_nT)

        def conv_stream(i0, n, extra=()):
            c = chan("cv%d" % i0)
            S.add("pool", (lambda e: e.dma_start(out=wsc_d[i0:i0 + n].rearrange("i p n -> p i n"),
                                                in_=wst_d[i0:i0 + n].rearrange("i p n -> p i n"))),
                  writes=[b_wsc[i] for i in range(i0, i0 + n)], chan=c, extra=extra)

        def conv_res(i0, n):
            c = chan("cr%d" % i0)
            if i0 < 8:
                dst, bd = wo_r[:, i0:i0 + n, :], b_wo[i0:i0 + n]
            else:
                dst, bd = wd_r[:, i0 - 8:i0 - 8 + n, :], b_wd[i0 - 8:i0 - 8 + n]
            S.add("pool", (lambda e: e.dma_start(out=dst, in_=wres_d[i0:i0 + n].rearrange("i p n -> p i n"))),
                  writes=bd, chan=c)

        S.add("pool", lambda e: e.memset(mhalf[:], -0.5), writes=[b_small])
        S.add("pool", lambda e: e.memset(ones64[:], 1.0), writes=[b_small])
        S.add("pool", lambda e: e.memset(ccx[:, :, 0:2], 0.0), writes=b_ccx)
        S.add("act", lambda e: e.activation(out=sk[:], in_=cc(C_SK, 4), func=AF.Exp), reads=[b_cst], writes=[b_sk])
        CG = 4
        groups = [(i, min(CG, NI - i)) for i in range(0, NI, CG)]
        first_x = [xh_load(0, s) for s in range(NSUB)]
        conv_todo = []
        for (i0, n) in groups:
            if i0 < 44:
                conv_todo.append((i0 + n - 1, (lambda i0=i0, n=n: conv_stream(i0, n, extra=first_x if i0 >= 8 else ()))))
        conv_todo.append((None, lambda: conv_res(0, 8)))
        for (i0, n) in groups:
            if i0 >= 44:
                conv_todo.append((i0 + n - 1, (lambda i0=i0, n=n: conv_stream(i0, n))))
        conv_todo.append((None, lambda: conv_res(8, 8)))
        conv_todo.append((None, lambda: conv_res(16, 8)))
        conv_todo.append((None, lambda: conv_res(24, 6)))
        conv_pos = [0]
        conv_done_item = [-1]
        CONV_PHASE2_PAIR = 12

        def ensure_conv(item):
            while conv_pos[0] < len(conv_todo) and (item is None or conv_done_item[0] < min(item, NI - 1)):
                last, th = conv_todo[conv_pos[0]]
                conv_pos[0] += 1
                th()
                if last is not None:
                    conv_done_item[0] = last

        ensure_conv(7)
        for s in range(NSUB):
            n1_A1(0, s)

        store_ops = []
        for pr in range(RS):
            ring_load(0, pr)
        next_load = [RS]

        def consumed_pair():
            g = next_load[0]
            if g < ntiles * NPAIR:
                ring_load(g // NPAIR, g % NPAIR)
                next_load[0] = g + 1

        for s in range(NSUB):
            n1_A2(0, s)
        for s in range(NSUB):
            n1_B(0, s)

        for t in range(ntiles):
            tok0 = t * T
            if t > 0:
                S.add("pool", lambda e: e.tensor_copy(out=KT[:, 0:128], in_=KT[:, T:T + 128]), reads=[b_KT], writes=[b_KT])
                S.add("pool", lambda e: e.tensor_copy(out=Vt[:, 0, :], in_=Vt[:, 4, :]), reads=[b_V], writes=[b_V])

            def fm_item(it, rhsT=nT, b_rhs=b_nT, split=False):
                ap, bring = item_ap(t, it)
                bi = nextbank()
                if split:
                    p1 = [(ap[:, kc * 128:(kc + 1) * 128], rhsT[:, kc, 0:256], ps[:, bi * 512:bi * 512 + 256], kc == 0, kc == 7)
                          for kc in range(8)]
                    p2 = [(ap[:, kc * 128:(kc + 1) * 128], rhsT[:, kc, 256:512], ps[:, bi * 512 + 256:(bi + 1) * 512], kc == 0, kc == 7)
                          for kc in range(8)]
                    return bi, (lambda: mm_op(None, p1, [bring] + b_hnT[0:2] + hn_al(0) + hn_al(1), bi)), \
                        (lambda: mm_op(None, p2, [bring] + b_hnT[2:4] + hn_al(2) + hn_al(3), bi))
                lrs = [(ap[:, kc * 128:(kc + 1) * 128], rhsT[:, kc, :]) for kc in range(8)]
                mm_op(bank_ap(bi), chain(lrs), [bring] + list(b_rhs), bi)
                if it % 2 == 1:
                    consumed_pair()
                return bi

            it = 0
            for g in range(4):
                bi = fm_item(it)
                bcol = C_BIAS + ITEMS[it][2]
                S.add("act", (lambda e, bi=bi, g=g, bcol=bcol: e.activation(out=QT[:, g, :], in_=bank_ap(bi), func=AF.Identity,
                                                                           bias=cc(bcol), scale=1.0)),
                      reads=[banks[bi], b_cst], writes=[b_QT[g]])
                it += 1
            bi = fm_item(it)
            bcol = C_BIAS + ITEMS[it][2]
            S.add("act", (lambda e, bi=bi, bcol=bcol: e.activation(out=KT[:, 128:128 + T], in_=bank_ap(bi), func=AF.Identity,
                                                                    bias=cc(bcol), scale=1.0)),
                  reads=[banks[bi], b_cst], writes=[b_KT])
            it += 1
            ap, bring = item_ap(t, it)
            bi = nextbank()
            prs = []
            for s in range(NSUB):
                for kc in range(8):
                    prs.append((nT[:, kc, s * 128:(s + 1) * 128], ap[:, kc * 128:(kc + 1) * 128],
                                ps[:, bi * 512 + s * 128: bi * 512 + (s + 1) * 128], kc == 0, kc == 7))
            mm_op(None, prs, [bring] + b_nT, bi)
            consumed_pair()
            S.add("dve", (lambda e, bi=bi: e.tensor_tensor(
                out=Vt[:, 1:5, :], in0=bank_ap(bi).rearrange("p (s n) -> p s n", s=4),
                in1=cc(C_BV, 128).unsqueeze(1).to_broadcast([128, 4, 128]), op=ALU.add)),
                reads=[banks[bi], b_cst], writes=[b_V])
            it += 1

            if t > 0:
                for s in range(NSUB):
                    xh_load(t, s)

            def attn_scores(qb):
                gq = t * 4 + qb
                k = gq % 2
                pcs = [1] if gq == 0 else [0, 1]
                for pc in pcs:
                    for kv in range(2):
                        bi = nextbank()
                        kcol = (qb + pc) * 128
                        lhsT = KT[kv * 64:(kv + 1) * 64, kcol:kcol + 128]
                        rhs = QT[kv * 64:(kv + 1) * 64, :, qb * 128:(qb + 1) * 128]
                        mm_op(bank_ap(bi), [(lhsT, rhs, None, True, True)], [b_KT] + b_QT, bi)
                        idx = pc * 2 + kv
                        S.add("act", (lambda e, bi=bi, k=k, idx=idx: e.activation(
                            out=PT[k][:, idx * 512:(idx + 1) * 512], in_=bank_ap(bi), func=AF.Exp, scale=0.125)),
                            reads=[banks[bi]], writes=[b_PT[k][pc][kv]])
                for pc in pcs:
                    m = maskp if pc == 0 else maskc
                    S.add("dve" if pc == 0 else "pool", (lambda e, k=k, pc=pc, m=m: e.tensor_tensor(
                        out=PT[k][:, pc * 1024:(pc + 1) * 1024].rearrange("p (h q) -> p h q", h=8),
                        in0=PT[k][:, pc * 1024:(pc + 1) * 1024].rearrange("p (h q) -> p h q", h=8),
                        in1=m.unsqueeze(1).to_broadcast([128, 8, 128]), op=ALU.mult)),
                        reads=[b_PT[k][pc][0], b_PT[k][pc][1], b_cbf], writes=[b_PT[k][pc][0], b_PT[k][pc][1]])

            def attn_pv(qb):
                gq = t * 4 + qb
                k = gq % 2
                pcs = [1] if gq == 0 else [0, 1]
                rd = [b_V, b_small] + [b_PT[k][pc][kv] for pc in pcs for kv in range(2)]
                bu = nextbank()
                prs = []
                for kv in range(2):
                    for n_, pc in enumerate(pcs):
                        idx = pc * 2 + kv
                        prs.append((Vt[:, qb + pc, kv * 64:(kv + 1) * 64], PT[k][:, idx * 512:(idx + 1) * 512],
                                    ps[kv * 64:(kv + 1) * 64, bu * 512:(bu + 1) * 512], n_ == 0, n_ == len(pcs) - 1))
                mm_op(None, prs, rd, bu)
                bd = nextbank()
                prs = []
                for kv in range(2):
                    for n_, pc in enumerate(pcs):
                        idx = pc * 2 + kv
                        prs.append((ones64[:, :], PT[k][:, idx * 512:(idx + 1) * 512],
                                    ps[kv * 64:(kv + 1) * 64, bd * 512:(bd + 1) * 512], n_ == 0, n_ == len(pcs) - 1))
                mm_op(None, prs, rd, bd)
                tp, btp = nexttmp()

                def lnfn(e, bd=bd, tp=tp):
                    ins = None
                    for g in range(4):
                        ins = e.activation(out=tp[:, g * 128:(g + 1) * 128], in_=bank_ap(bd)[:, g * 128:(g + 1) * 128],
                                           func=AF.Ln, bias=sk[:, g:g + 1], scale=1.0)
                    return ins
                S.add("act", lnfn, reads=[banks[bd], b_sk], writes=[btp])
                S.add("act", (lambda e, tp=tp: e.activation(out=tp[:], in_=tp[:], func=AF.Exp, scale=-1.0)),
                      reads=[btp], writes=[btp])
                S.add("dve", (lambda e, bu=bu, tp=tp, qb=qb: e.tensor_tensor(
                    out=attnT[:, :, qb * 128:(qb + 1) * 128], in0=bank_ap(bu).rearrange("p (g q) -> p g q", g=4),
                    in1=tp[:].rearrange("p (g q) -> p g q", g=4), op=ALU.mult)),
                    reads=[banks[bu], btp], writes=[b_attnT[qb]])

            def conv_chunk(c):
                nonlocal it
                bx = fm_item(it)
                bcol = C_BIAS + ITEMS[it][2]
                t1, bt1 = nexttmp()
                S.add("act", (lambda e, bx=bx, t1=t1, bcol=bcol: e.activation(out=t1[:], in_=bank_ap(bx), func=AF.Identity,
                                                                             bias=cc(bcol), scale=1.0)),
                      reads=[banks[bx], b_cst], writes=[bt1])
                it += 1
                bc_ = fm_item(it)
                bcol = C_BIAS + ITEMS[it][2]
                S.add("dve", (lambda e, bc_=bc_, t1=t1, bcol=bcol, c=c: e.scalar_tensor_tensor(
                    out=ccx[:, c, 2:T + 2], in0=bank_ap(bc_), scalar=cc(bcol), in1=t1[:], op0=ALU.add, op1=ALU.mult)),
                    reads=[banks[bc_], b_cst, bt1], writes=[b_ccx[c]])
                it += 1
                a, ba = nexttmp()
                S.add("pool", (lambda e, a=a, c=c: e.tensor_scalar(out=a[:], in0=ccx[:, c, 2:T + 2], scalar1=cc(C_CW + c * 3 + 2),
                                                                 scalar2=0.0, op0=ALU.mult, op1=ALU.add)),
                      reads=[b_ccx[c], b_cst], writes=[ba])
                for kk in (1, 0):
                    S.add("dve", (lambda e, a=a, c=c, kk=kk: e.scalar_tensor_tensor(
                        out=a[:], in0=ccx[:, c, kk:kk + T], scalar=cc(C_CW + c * 3 + kk), in1=a[:], op0=ALU.mult, op1=ALU.add)),
                        reads=[b_ccx[c], b_cst, ba], writes=[ba])
                S.add("pool", (lambda e, c=c: e.tensor_copy(out=ccx[:, c, 0:2], in_=ccx[:, c, T:T + 2])),
                      reads=[b_ccx[c]], writes=[b_ccx[c]])
                bb = fm_item(it)
                bcol = C_BIAS + ITEMS[it][2]
                S.add("dve", (lambda e, bb=bb, a=a, bcol=bcol, c=c: e.scalar_tensor_tensor(
                    out=convT[:, c, :], in0=bank_ap(bb), scalar=cc(bcol), in1=a[:], op0=ALU.add, op1=ALU.mult)),
                    reads=[banks[bb], b_cst, ba], writes=[b_convT[c]] + b_ctok)
                it += 1

            attn_scores(0)
            for c in range(4):
                conv_chunk(c)
                if c + 1 < 4:
                    attn_scores(c + 1)
                attn_pv(c)

            for c in range(8):
                bA = fm_item(it)
                bcolA = C_BIAS + ITEMS[it][2]
                it += 1
                bB = fm_item(it)
                bcolB = C_BIAS + ITEMS[it][2]
                it += 1
                ap, bring = item_ap(t, it)
                bC = nextbank()
                mm_op(bank_ap(bC), chain([(ap[:, g * 128:(g + 1) * 128], attnT[:, g, :]) for g in range(4)]),
                      [bring] + b_attnT, bC)
                bD = nextbank()
                mm_op(bank_ap(bD), chain([(ap[:, (4 + g) * 128:(5 + g) * 128], convT[:, g, :]) for g in range(4)]),
                      [bring] + b_convT + b_ctok, bD)
                if it % 2 == 1:
                    consumed_pair()
                it += 1
                t1, bt1 = nexttmp()
                t2, bt2 = nexttmp()
                S.add("act", (lambda e, bA=bA, t1=t1, bcolA=bcolA: e.activation(out=t1[:], in_=bank_ap(bA), func=AF.Sigmoid,
                                                                               bias=cc(bcolA), scale=1.0)),
                      reads=[banks[bA], b_cst], writes=[bt1])
                S.add("act", (lambda e, bB=bB, t2=t2, bcolB=bcolB: e.activation(out=t2[:], in_=bank_ap(bB), func=AF.Sigmoid,
                                                                               bias=cc(bcolB), scale=1.0)),
                      reads=[banks[bB], b_cst], writes=[bt2])
                S.add("dve", (lambda e, bC=bC, t1=t1: e.tensor_tensor(out=t1[:], in0=bank_ap(bC), in1=t1[:], op=ALU.mult)),
                      reads=[banks[bC], bt1], writes=[bt1])
                S.add("dve", (lambda e, bD=bD, t2=t2: e.tensor_tensor(out=t2[:], in0=bank_ap(bD), in1=t2[:], op=ALU.mult)),
                      reads=[banks[bD], bt2], writes=[bt2])
                S.add("pool", (lambda e, t1=t1, t2=t2, c=c: e.tensor_tensor(out=mergedT[:, c, :], in0=t1[:], in1=t2[:], op=ALU.add)),
                      reads=[bt1, bt2], writes=[b_merged[c]])
            assert it == NI_MIX

            if t == 0:
                ensure_conv(47)
            hk = {}

            def wout_mm(s, mid=None):
                bis_ = [nextbank(), nextbank()]
                lr = [[(mergedT[:, kc, s * 128:(s + 1) * 128], wo_r[:, kc, hh * 512:(hh + 1) * 512]) for kc in range(8)]
                      for hh in range(2)]

                def add(hh):
                    bi = bis_[hh]
                    S.add("dve", (lambda e, bi=bi, s=s, hh=hh: e.tensor_tensor(
                        out=xh[:, s, hh * 512:(hh + 1) * 512], in0=bank_ap(bi), in1=xh[:, s, hh * 512:(hh + 1) * 512], op=ALU.add)),
                        reads=[banks[bi], b_xh[s]], writes=[b_xh[s]])
                if s == 0:
                    for (a0, a1) in ((0, 6), (6, 7), (7, 8)):
                        for hh in range(2):
                            mm_op(bank_ap(bis_[hh]), chain(lr[hh][a0:a1], first=(a0 == 0), last=(a1 == 8)),
                                  b_merged[a0:a1] + b_wo[a0:a1], bis_[hh])
                    add(0); add(1)
                elif mid is not None:
                    mm_op(bank_ap(bis_[0]), chain(lr[0]), b_merged + b_wo, bis_[0])
                    add(0)
                    mid()
                    mm_op(bank_ap(bis_[1]), chain(lr[1]), b_merged + b_wo, bis_[1])
                    add(1)
                else:
                    for hh in range(2):
                        mm_op(bank_ap(bis_[hh]), chain(lr[hh]), b_merged + b_wo, bis_[hh])
                    add(0); add(1)

            def n2_stats(s):
                norm_stats(xh[:, s, :], b_xh[s], 4 + s)

            def n2_scale(s):
                hk[s] = norm_scale(xh[:, s, :], b_xh[s], 4 + s)

            def n2_T(s):
                norm_transpose(hk[s], C_G2, s, hnT, b_hnT, extra_w=hn_al(s))

            wout_mm(0); n2_stats(0)
            wout_mm(1); n2_stats(1); n2_scale(0)
            wout_mm(2); n2_scale(1); n2_stats(2)
            n2_T(0)
            wout_mm(3, mid=lambda: n2_T(1))
            n2_scale(2)
            n2_stats(3)
            n2_scale(3)

            if t > 0:
                def fwv(k):
                    return cst[:, C_FW:C_FW + 132].rearrange("p (m k) -> p m k", k=3)[:, :, k]
                S.add("pool", lambda e: e.tensor_tensor(out=fx[:, :, 0], in0=uh[:, :, 0], in1=fwv(0), op=ALU.mult),
                      reads=[b_uh, b_cst], writes=[b_fx])
                S.add("pool", lambda e: e.tensor_tensor(out=fxt[:], in0=uh[:, :, 1], in1=fwv(1), op=ALU.mult),
                      reads=[b_uh, b_cst], writes=[b_fxt])
                S.add("pool", lambda e: e.tensor_tensor(out=fx[:, :, 0], in0=fx[:, :, 0], in1=fxt[:], op=ALU.add),
                      reads=[b_fx, b_fxt], writes=[b_fx])
                S.add("pool", lambda e: e.tensor_tensor(out=fx[:, :, 1], in0=uh[:, :, 1], in1=fwv(0), op=ALU.mult),
                      reads=[b_uh, b_cst], writes=[b_fx])

            hooks = {}
            if t + 1 < ntiles:
                for s in range(NSUB):
                    hooks.setdefault(1 + 2 * s, []).append(lambda s=s: n1_A1(t + 1, s))
                    hooks.setdefault(3 + 2 * s, []).append(lambda s=s: n1_A2(t + 1, s))
                    hooks.setdefault(6 + 2 * s, []).append(lambda s=s: n1_B(t + 1, s))

            pending = None
            for j in range(22):
                accs = []
                bis = []
                if j < 2 and False:
                    pass
                if j == 0:
                    late = []
                    split_bis = []
                    n2_T(2)
                    for q in range(4):
                        bi_, f1, f2 = fm_item(it + q, hnT, b_hnT + hn_alias, split=True)
                        f1()
                        late.append(f2)
                        split_bis.append(bi_)
                        if q == 2:
                            n2_T(3)
                    for q, f2 in enumerate(late):
                        f2()
                        if q % 2 == 1:
                            consumed_pair()
                if j < 2:
                    for half in range(2):
                        bis.append(split_bis[2 * j + half])
                        it += 1
                else:
                    for half in range(2):
                        bis.append(fm_item(it, hnT, b_hnT + hn_alias))
                        it += 1
                for half in range(2):
                    m = 2 * j + half
                    bi = bis[half]
                    a, ba = nexttmp()
                    fcol = C_FW + m * 3
                    S.add("act", (lambda e, bi=bi, a=a, fcol=fcol: e.activation(out=a[:], in_=bank_ap(bi), func=AF.Identity,
                                                                               scale=cc(fcol + 2))),
                          reads=[banks[bi], b_cst], writes=[ba])
                    if t + 1 < ntiles:
                        S.add("act", (lambda e, bi=bi, m=m: e.activation(out=uh[:, m, :], in_=bank_ap(bi)[:, T - 2:T], func=AF.Copy)),
                              reads=[banks[bi]], writes=[b_uh])
                    if t > 0:
                        S.add("pool", (lambda e, a=a, m=m: e.tensor_tensor(out=a[:, 0:2], in0=a[:, 0:2], in1=fx[:, m, :], op=ALU.add)),
                              reads=[ba, b_fx], writes=[ba])
                    accs.append((a, ba, bi, fcol))
                for (kk, lo) in ((1, 1), (0, 2)):
                    for (a, ba, bi, fcol) in accs:
                        S.add("dve", (lambda e, bi=bi, a=a, fcol=fcol, kk=kk, lo=lo: e.scalar_tensor_tensor(
                            out=a[:, lo:T], in0=bank_ap(bi)[:, 0:T - lo], scalar=cc(fcol + kk), in1=a[:, lo:T],
                            op0=ALU.mult, op1=ALU.add)),
                            reads=[banks[bi], b_cst, ba], writes=[ba])
                (ag, bag, _, _), (av, bav, _, _) = accs

                def finish(ag=ag, bag=bag, av=av, bav=bav, j=j):
                    S.add("act", (lambda e: e.activation(out=ag[:], in_=ag[:], func=AF.Silu)), reads=[bag], writes=[bag])
                    S.add("pool", (lambda e: e.tensor_tensor(out=actT[:, j, :], in0=ag[:], in1=av[:], op=ALU.mult)),
                          reads=[bag, bav], writes=[b_actT[j]])
                if pending is not None:
                    pending()
                pending = finish
                for h in hooks.get(j, []):
                    h()
            pending()
            assert it == NI
            if t + 1 < ntiles:
                S.add("act", lambda e: e.activation(out=dummy[:, 0:1], in_=mhalf[:, 0:1], func=AF.Exp),
                      reads=[b_small], writes=[b_dummy])
            if t == 0:
                ensure_conv(None)

            for s in range(NSUB):
                for hh in range(2):
                    bi = nextbank()
                    lrs = [(actT[:, j, s * 128:(s + 1) * 128], wd_r[:, j, hh * 512:(hh + 1) * 512]) for j in range(22)]
                    if s == 0 and hh == 0:
                        cuts = [0, 12, 16, 19, 21, 22]
                        for ci in range(len(cuts) - 1):
                            a0, a1 = cuts[ci], cuts[ci + 1]
                            mm_op(bank_ap(bi), chain(lrs[a0:a1], first=(a0 == 0), last=(a1 == 22)),
                                  b_actT[a0:a1] + b_wd[a0:a1], bi)
                    else:
                        mm_op(bank_ap(bi), chain(lrs), b_actT + b_wd, bi)
                    S.add("dve", (lambda e, bi=bi, s=s, hh=hh: e.tensor_tensor(
                        out=xh[:, s, hh * 512:(hh + 1) * 512], in0=bank_ap(bi), in1=xh[:, s, hh * 512:(hh + 1) * 512], op=ALU.add)),
                        reads=[banks[bi], b_xh[s]], writes=[b_xh[s]])
                norm_stats(xh[:, s, :], b_xh[s], 8 + s)
                k = nextio()
                S.add("dve", (lambda e, k=k, s=s: e.scalar_tensor_tensor(
                    out=io[k][:], in0=xh[:, s, :], scalar=rs[:, 8 + s:9 + s], in1=cc(C_GF, 1024), op0=ALU.mult, op1=ALU.mult)),
                    reads=[b_xh[s], b_rs[8 + s], b_cst], writes=[b_io[k]])
                r0 = tok0 + s * 128
                store_ops.append(S.add("sp", (lambda e, k=k, r0=r0: e.dma_start(out=out_d[r0:r0 + 128, :], in_=io[k][:])),
                                       reads=[b_io[k]], chan=c_io_out[k]))

        S.add("sp", lambda e: e.nop(), extra=store_ops)
        S.run(nc, sems)
    return nc


def _fm_item(Wcols):
    return np.ascontiguousarray(Wcols.reshape(8, 128, 128).transpose(1, 0, 2).reshape(128, 1024))


def prepare_weights(mix_norm, w_in, b_in, sinks, conv_w, w_attn_branch, w_conv_branch, w_out,
                    ffn_norm, w_up, ffn_conv_w, w_down, final_norm):
    f32 = np.float32
    w_in = np.asarray(w_in, f32)[0]; b_in = np.asarray(b_in, f32)[0]
    wa = np.asarray(w_attn_branch, f32)[0]; wc = np.asarray(w_conv_branch, f32)[0]
    wo = np.asarray(w_out, f32)[0]; wu = np.asarray(w_up, f32)[0]; wd = np.asarray(w_down, f32)[0]
    cw = np.asarray(conv_w, f32)[0]; fcw = np.asarray(ffn_conv_w, f32)[0]
    sinks = np.asarray(sinks, f32)[0]
    ar = np.arange(128)
    kvd = np.array([kv * 256 + d for kv in range(2) for d in range(64)])

    def cols_of(kind, idx):
        if kind == "q":
            return kvd + idx * 64
        if kind == "k":
            return 512 + ar
        if kind == "v":
            return 640 + ar
        if kind == "cb":
            return 768 + idx * 128 + ar
        if kind == "cc":
            return 1280 + idx * 128 + ar
        if kind == "cx":
            return 1792 + idx * 128 + ar
        if kind == "ga":
            return 2304 + idx * 128 + ar
        if kind == "gc":
            return 3328 + idx * 128 + ar
        raise ValueError(kind)

    wst = np.zeros((NI, 128, 1024), f32)
    cst = np.zeros((128, NCST), f32)
    for i, (kind, idx, bcol) in enumerate(ITEMS):
        if kind == "wac":
            c = idx
            for g in range(4):
                wst[i][:, g * 128:(g + 1) * 128] = wa[kvd + g * 64, c * 128:(c + 1) * 128]
                wst[i][:, (4 + g) * 128:(5 + g) * 128] = wc[g * 128 + ar, c * 128:(c + 1) * 128]
        elif kind == "fg":
            wst[i] = _fm_item(wu[:, idx * 128:(idx + 1) * 128])
        elif kind == "fv":
            wst[i] = _fm_item(wu[:, 2816 + idx * 128:2816 + (idx + 1) * 128])
        else:
            cols = cols_of(kind, idx)
            wst[i] = _fm_item(w_in[:, cols])
            cst[:, C_BIAS + bcol] = b_in[cols]
    wres = np.zeros((NRES, 128, 1024), f32)
    for kc in range(8):
        wres[kc] = wo[kc * 128:(kc + 1) * 128, :]
    for j in range(22):
        wres[8 + j] = wd[j * 128:(j + 1) * 128, :]
    cst[:, C_G1:C_G1 + 8] = np.asarray(mix_norm, f32)[0].reshape(8, 128).T
    cst[:, C_G2:C_G2 + 8] = np.asarray(ffn_norm, f32)[0].reshape(8, 128).T
    for c in range(4):
        for k in range(3):
            cst[:, C_CW + c * 3 + k] = cw[k, c * 128:(c + 1) * 128]
    for j in range(22):
        for half in range(2):
            m = 2 * j + half
            base = j * 128 if half == 0 else 2816 + j * 128
            for k in range(3):
                cst[:, C_FW + m * 3 + k] = fcw[k, base:base + 128]
    for p in range(128):
        cst[p, C_SK:C_SK + 4] = sinks[(p // 64) * 4:(p // 64) * 4 + 4]
    cst[:, C_BV:C_BV + 128] = b_in[640:768][None, :]
    cst[:, C_GF:C_GF + 1024] = np.asarray(final_norm, f32)[None, :]
    bf = ml_dtypes.bfloat16
    cbf = np.zeros((128, 384), bf)
    cbf[:, 0:128] = np.eye(128).astype(bf)
    kk = np.arange(128)[:, None]; qq = np.arange(128)[None, :]
    cbf[:, 128:256] = (kk > qq).astype(bf)
    cbf[:, 256:384] = (kk <= qq).astype(bf)
    return wst, wres, cst, cbf


_NC_CACHE = {}


def run(x, weights, ntiles):
    wst, wres, cst, cbf = weights
    B = x.shape[0]
    if ntiles not in _NC_CACHE:
        _NC_CACHE[ntiles] = build_nc(ntiles)
    nc = _NC_CACHE[ntiles]
    in_maps = [{"x": np.ascontiguousarray(x[b]), "wst": wst, "wres": wres, "cst": cst, "cbf": cbf} for b in range(B)]
    res = run_bass_kernel_spmd(nc, in_maps, core_ids=list(range(B)))
    return np.stack([res.results[b]["out"] for b in range(B)], axis=0)


def kernel(x, mix_norm, w_in, b_in, sinks, conv_w, w_attn_branch, w_conv_branch, w_out,
           ffn_norm, w_up, ffn_conv_w, w_down, final_norm):
    x = np.asarray(x, np.float32)
    weights = prepare_weights(mix_norm, w_in, b_in, sinks, conv_w, w_attn_branch, w_conv_branch, w_out,
                              ffn_norm, w_up, ffn_conv_w, w_down, final_norm)
    assert x.shape[1] % T == 0
    return run(x, weights, x.shape[1] // T).astype(np.float32)
```
